# Optimizing a Trainium2 kernel written in Bass

```python
import math
import jax, jax.numpy as jnp
from jax import lax
import numpy as np

D_MODEL = 2048
BATCH = 1
SEQ = 16384
DEPTH = 2

MIX_WIDTH = D_MODEL
POOL_WIDTH = D_MODEL // 4
POOL_WINDOWS = (2, 4, 8, 16)
POOL_GROUP = POOL_WIDTH // len(POOL_WINDOWS)
LRU_WIDTH = D_MODEL // 4
LRU_HEADS = 8
LRU_HEAD_DIM = LRU_WIDTH // LRU_HEADS
LRU_CONV = 4
LRU_C = 8.0
ATTN_WIDTH = D_MODEL // 2
ATTN_HEAD_DIM = 64
ATTN_HEADS = ATTN_WIDTH // ATTN_HEAD_DIM
ATTN_BRANCHES = ((128, 1), (512, 4), (2048, 16))
ATTN_BLOCK = 128
D_FF = 5632
IN_WIDTH = POOL_WIDTH + 2 * LRU_WIDTH + 3 * ATTN_WIDTH
IN_SPLITS = (POOL_WIDTH, POOL_WIDTH + LRU_WIDTH, POOL_WIDTH + 2 * LRU_WIDTH,
             POOL_WIDTH + 2 * LRU_WIDTH + ATTN_WIDTH,
             POOL_WIDTH + 2 * LRU_WIDTH + 2 * ATTN_WIDTH)
NORM_EPS = 1e-6
NEG_INF = -1e30

kernel_name = 'hybrid_pool_rglru_dilated_attn_block'


def rms_norm(x, g):
    xf = x.astype(jnp.float32)
    y = xf * lax.rsqrt(jnp.mean(xf * xf, axis=-1, keepdims=True) + NORM_EPS)
    return (y * g.astype(jnp.float32)).astype(x.dtype)


def swiglu(x, w_in, w_out):
    gate, up = jnp.split(x @ w_in, 2, axis=-1)
    return (jax.nn.silu(gate) * up) @ w_out


def alibi_slopes(n_heads):
    return jnp.asarray(2.0 ** (-8.0 * np.arange(1, n_heads + 1) / n_heads), dtype=jnp.float32)


def pool_mixer(u, w, scale):
    b, s, _ = u.shape
    uf = u.astype(jnp.float32).reshape(b, s, len(POOL_WINDOWS), POOL_GROUP)
    cs = jnp.cumsum(uf, axis=1)
    pos = jnp.arange(1, s + 1, dtype=jnp.float32)
    means = []
    for gi, win in enumerate(POOL_WINDOWS):
        c = cs[:, :, gi]
        lag = jnp.pad(c, ((0, 0), (win, 0), (0, 0)))[:, :s]
        means.append((c - lag) / jnp.minimum(pos, float(win))[:, None])
    pooled = jnp.stack(means, axis=2) - uf
    y = jnp.einsum('bsgc,gcd->bsgd', pooled, w.astype(jnp.float32))
    y = y.reshape(b, s, POOL_WIDTH) * scale.astype(jnp.float32)
    return y.astype(u.dtype)


def _lru_combine(left, right):
    a1, b1 = left
    a2, b2 = right
    return a1 * a2, a2 * b1 + b2


def rg_lru_mixer(xb, gate_in, conv_w, conv_b, gate_w, gate_b, lam):
    b, s, c = xb.shape
    xc = lax.conv_general_dilated(
        xb, conv_w[:, None, :], window_strides=(1,), padding=((LRU_CONV - 1, 0),),
        dimension_numbers=('NWC', 'WIO', 'NWC'), feature_group_count=c) + conv_b
    xh = xc.reshape(b, s, LRU_HEADS, LRU_HEAD_DIM)
    g = jnp.einsum('bshc,ghcd->gbshd', xh, gate_w) + gate_b[:, None, None]
    r = jax.nn.sigmoid(g[0].astype(jnp.float32)).reshape(b, s, c)
    i = jax.nn.sigmoid(g[1].astype(jnp.float32)).reshape(b, s, c)
    log_a = -LRU_C * r * jax.nn.softplus(-lam.astype(jnp.float32))
    a = jnp.exp(log_a)
    u = jnp.sqrt(-jnp.expm1(2.0 * log_a)) * (i * xc.astype(jnp.float32))
    _, h = lax.associative_scan(_lru_combine, (a, u), axis=1)
    y = h * jax.nn.gelu(gate_in.astype(jnp.float32))
    return y.astype(xb.dtype)


def _dilated_branch(q, k, v, window, dilation, slopes):
    b, s, h, dh = q.shape
    span = window // dilation
    unit = dilation * ATTN_BLOCK
    s_pad = -(-s // unit) * unit
    n_sub = s_pad // dilation
    nb = n_sub // ATTN_BLOCK
    pad = ((0, 0), (0, s_pad - s), (0, 0), (0, 0))

    def to_blocks(t):
        t = jnp.pad(t, pad).reshape(b, n_sub, dilation, h, dh)
        t = jnp.transpose(t, (0, 2, 1, 3, 4))
        return t.reshape(b, dilation, nb, ATTN_BLOCK, h, dh)

    def with_prev(t):
        prev = jnp.pad(t, ((0, 0), (0, 0), (1, 0), (0, 0), (0, 0), (0, 0)))[:, :, :-1]
        return jnp.concatenate([prev, t], axis=3)

    qb = to_blocks(q)
    kb = with_prev(to_blocks(k))
    vb = with_prev(to_blocks(v)).astype(jnp.float32)
    scores = jnp.einsum('brnqhc,brnkhc->brnhqk', qb, kb).astype(jnp.float32)
    qi = jnp.arange(ATTN_BLOCK)[:, None]
    ki = jnp.arange(2 * ATTN_BLOCK)[None, :]
    dist = qi + ATTN_BLOCK - ki
    first = (jnp.arange(nb) == 0)[:, None, None] & (ki < ATTN_BLOCK)[None]
    valid = ((dist >= 0) & (dist <= span))[None] & ~first
    bias = -slopes[:, None, None] * (dist * dilation).astype(jnp.float32)[None]
    scores = jnp.where(valid[None, None, :, None], scores + bias, NEG_INF)
    m = jnp.max(scores, axis=-1)
    p = jnp.exp(scores - m[..., None])
    l = jnp.sum(p, axis=-1)
    m = jnp.transpose(m, (0, 1, 2, 4, 3))
    l = jnp.transpose(l, (0, 1, 2, 4, 3))
    o = jnp.einsum('brnhqk,brnkhc->brnqhc', p, vb) / l[..., None]

    def from_blocks(t):
        tail = t.shape[5:]
        t = t.reshape((b, dilation, n_sub, h) + tail)
        t = jnp.moveaxis(t, 1, 2)
        return t.reshape((b, s_pad, h) + tail)[:, :s]

    return from_blocks(o), from_blocks(m), from_blocks(l)


def dilated_attention(q, k, v):
    slopes = alibi_slopes(q.shape[2])
    outs, ms, ls = [], [], []
    for window, dilation in ATTN_BRANCHES:
        o, m, l = _dilated_branch(q, k, v, window, dilation, slopes)
        outs.append(o)
        ms.append(m)
        ls.append(l)
    m_all = jnp.stack(ms)
    wts = jnp.stack(ls) * jnp.exp(m_all - jnp.max(m_all, axis=0, keepdims=True))
    o = jnp.sum(wts[..., None] * jnp.stack(outs), axis=0) / jnp.sum(wts, axis=0)[..., None]
    return o.astype(q.dtype)


def hybrid_mixer(h, w_in, w_out, pool_w, pool_scale, conv_w, conv_b, gate_w, gate_b, lam):
    b, s, _ = h.shape
    z = h @ w_in
    pool_in, lru_x, lru_g, q, k, v = jnp.split(z, IN_SPLITS, axis=-1)
    y_pool = pool_mixer(pool_in, pool_w, pool_scale)
    y_lru = rg_lru_mixer(lru_x, lru_g, conv_w, conv_b, gate_w, gate_b, lam)
    q = q.reshape(b, s, ATTN_HEADS, ATTN_HEAD_DIM) * (ATTN_HEAD_DIM ** -0.5)
    k = k.reshape(b, s, ATTN_HEADS, ATTN_HEAD_DIM)
    v = v.reshape(b, s, ATTN_HEADS, ATTN_HEAD_DIM)
    y_attn = dilated_attention(q, k, v).reshape(b, s, ATTN_WIDTH)
    y = jnp.concatenate([y_pool, y_lru.astype(h.dtype), y_attn.astype(h.dtype)], axis=-1)
    return y @ w_out


def setup_inputs(seed: int = 0) -> dict:
    key = jax.random.key(seed)
    ks = jax.random.split(key, 16)
    f32 = jnp.float32

    def nrm(k, shape, scale):
        return jax.random.normal(k, shape, f32) * scale

    a0 = jax.random.uniform(ks[15], (DEPTH, LRU_WIDTH), f32, 0.9, 0.999)
    p0 = a0 ** (1.0 / LRU_C)
    lam = jnp.log(p0) - jnp.log1p(-p0)
    return {
        'x': nrm(ks[0], (BATCH, SEQ, D_MODEL), 1.0),
        'norm_g': 1.0 + nrm(ks[1], (DEPTH, 6, D_MODEL), 0.1),
        'ffn1_w_in': nrm(ks[2], (DEPTH, D_MODEL, 2 * D_FF), D_MODEL ** -0.5),
        'ffn1_w_out': nrm(ks[3], (DEPTH, D_FF, D_MODEL), D_FF ** -0.5),
        'mix_w_in': nrm(ks[4], (DEPTH, D_MODEL, IN_WIDTH), D_MODEL ** -0.5),
        'mix_w_out': nrm(ks[5], (DEPTH, MIX_WIDTH, D_MODEL), MIX_WIDTH ** -0.5),
        'pool_w': nrm(ks[6], (DEPTH, len(POOL_WINDOWS), POOL_GROUP, POOL_GROUP), POOL_GROUP ** -0.5),
        'pool_scale': 1.0 + nrm(ks[7], (DEPTH, POOL_WIDTH), 0.1),
        'lru_conv_w': nrm(ks[8], (DEPTH, LRU_CONV, LRU_WIDTH), LRU_CONV ** -0.5),
        'lru_conv_b': nrm(ks[9], (DEPTH, LRU_WIDTH), 0.1),
        'lru_gate_w': nrm(ks[10], (DEPTH, 2, LRU_HEADS, LRU_HEAD_DIM, LRU_HEAD_DIM), LRU_HEAD_DIM ** -0.5),
        'lru_gate_b': nrm(ks[11], (DEPTH, 2, LRU_HEADS, LRU_HEAD_DIM), 0.1),
        'lru_lambda': lam,
        'ffn2_w_in': nrm(ks[12], (DEPTH, D_MODEL, 2 * D_FF), D_MODEL ** -0.5),
        'ffn2_w_out': nrm(ks[13], (DEPTH, D_FF, D_MODEL), D_FF ** -0.5),
    }


def reference(x, norm_g, ffn1_w_in, ffn1_w_out, mix_w_in, mix_w_out, pool_w, pool_scale,
              lru_conv_w, lru_conv_b, lru_gate_w, lru_gate_b, lru_lambda, ffn2_w_in, ffn2_w_out):
    for layer in range(DEPTH):
        g = norm_g[layer]
        h = swiglu(rms_norm(x, g[0]), ffn1_w_in[layer], ffn1_w_out[layer])
        x = x + 0.5 * rms_norm(h, g[1])
        h = hybrid_mixer(rms_norm(x, g[2]), mix_w_in[layer], mix_w_out[layer], pool_w[layer],
                         pool_scale[layer], lru_conv_w[layer], lru_conv_b[layer],
                         lru_gate_w[layer], lru_gate_b[layer], lru_lambda[layer])
        x = x + rms_norm(h, g[3])
        h = swiglu(rms_norm(x, g[4]), ffn2_w_in[layer], ffn2_w_out[layer])
        x = x + 0.5 * rms_norm(h, g[5])
    return x
```

```python
import numpy as np
import ml_dtypes
from contextlib import ExitStack
import concourse.bass as bass
import concourse.mybir as mybir
from concourse.bass_utils import run_bass_kernel_spmd

F32 = mybir.dt.float32
BF16 = mybir.dt.bfloat16
I32 = mybir.dt.int32
AF = mybir.ActivationFunctionType
ALU = mybir.AluOpType

NCORES = 8
SEQ = 16384
TOK = SEQ // NCORES
T = 512
NT = TOK // T
D = 2048
KD = D // 128
DFF = 5632
KF = DFF // 128
EPS = 1e-6
SLAB = 8192
NSLOT = 3
ENGS = ("pe", "act", "dve", "pool", "sp")


class Sem:
    def __init__(self, h):
        self.h = h
        self.n = 0


class Res:
    __slots__ = ("w", "r")

    def __init__(self):
        self.w = None
        self.r = []


def RL(n):
    return [Res() for _ in range(n)]


def _flat(x):
    out = []
    if isinstance(x, Res):
        return [x]
    for e in x:
        if isinstance(e, (list, tuple)):
            out.extend(_flat(e))
        elif e is not None:
            out.append(e)
    return out


class Stage:
    uid = 0

    def __init__(self, nc, stack):
        self.nc = nc
        self.stack = stack
        self.ops = {e: [] for e in ENGS}
        Stage.uid += 1
        self.id = Stage.uid
        self.prog = {e: self.sem("pg" + e) for e in ENGS}
        self.known = {e: {} for e in ENGS}
        self.dsems = []

    def sem(self, name):
        return Sem(self.stack.enter_context(self.nc.semaphore(f"{name}_{self.id}")))

    def dsem(self, name):
        s = self.sem(name)
        self.dsems.append(s)
        return s

    def _deps(self, eng, rd, wr, extra=()):
        need = {}
        marks = list(extra)
        for r in _flat(rd):
            marks.append(r.w)
        for r in _flat(wr):
            marks.append(r.w)
            marks.extend(r.r)
        for m in marks:
            if m is None:
                continue
            s, v = m
            if v <= 0:
                continue
            if eng == "pe" and s is self.prog["pe"]:
                continue
            if self.known[eng].get(s, 0) >= v:
                continue
            if need.get(s, 0) < v:
                need[s] = v
        for s, v in need.items():
            self.known[eng][s] = v
        return tuple(need.items())

    def _upd(self, mark, rd, wr):
        for r in _flat(rd):
            r.r.append(mark)
        for r in _flat(wr):
            r.w = mark
            r.r = []

    def op(self, eng, name, *args, rd=(), wr=(), inc=True, **kw):
        waits = self._deps(eng, rd, wr)
        p = self.prog[eng]
        if inc:
            p.n += 1
            incr = (p, 1)
            mark = (p, p.n)
        else:
            incr = None
            mark = (p, p.n + 1)
        self.ops[eng].append((name, args, kw, waits, incr))
        self._upd(mark, rd, wr)
        return mark

    def dma(self, eng, out, in_, sem, rd=(), wr=()):
        waits = self._deps(eng, rd, wr, extra=((sem, sem.n),))
        sem.n += 16
        mark = (sem, sem.n)
        self.ops[eng].append(("dma_start", (), dict(out=out, in_=in_), waits, (sem, 16)))
        self._upd(mark, rd, wr)
        return mark

    def emit(self):
        fin = tuple((s, s.n) for s in self.dsems if s.n > 0)
        self.ops["sp"].append((None, (), {}, fin, None))
        ops = self.ops

        def mk(eng):
            def f(e):
                for (name, args, kw, waits, incr) in ops[eng]:
                    for (s, v) in waits:
                        e.wait_ge(s.h, v)
                    if name is None:
                        continue
                    ins = getattr(e, name)(*args, **kw)
                    if incr is not None:
                        ins.then_inc(incr[0].h, incr[1])
            return f

        with self.nc.Block() as blk:
            blk.tensor(mk("pe"))
            blk.scalar(mk("act"))
            blk.vector(mk("dve"))
            blk.gpsimd(mk("pool"))
            blk.sync(mk("sp"))


class WRing:
    def __init__(self, S, wring):
        self.S = S
        self.wring = wring
        self.ld = [S.dsem(f"wld{i}") for i in range(NSLOT)]
        self.res = RL(NSLOT)
        self.i = 0

    def load(self, src2d, n):
        slot = self.i % NSLOT
        self.i += 1
        self.S.dma("pool", self.wring[:, slot, 0:n], src2d, self.ld[slot], wr=self.res[slot])
        return slot, self.res[slot]


class Bufs:
    def __init__(self, nc, es):
        self.xs = es.enter_context(nc.sbuf_tensor("xs", [128, KD, T], F32))
        self.xn = es.enter_context(nc.sbuf_tensor("xn", [128, KD, T], BF16))
        self.hb = es.enter_context(nc.sbuf_tensor("hb", [128, KF, T], BF16))
        self.ys = es.enter_context(nc.sbuf_tensor("ys", [128, KD, T], F32))
        self.rstd = es.enter_context(nc.sbuf_tensor("rstd", [128, 2, T], F32))
        self.sg = es.enter_context(nc.sbuf_tensor("sg", [128, 2, T], F32))
        self.wring = es.enter_context(nc.sbuf_tensor("wring", [128, NSLOT, SLAB], BF16))
        self.ones = es.enter_context(nc.sbuf_tensor("ones", [128, 128], BF16))
        self.g = es.enter_context(nc.sbuf_tensor("gcol", [128, 6, KD], F32))
        self.gh = es.enter_context(nc.sbuf_tensor("gcolh", [128, 6, KD], F32))
        self.epsc = es.enter_context(nc.sbuf_tensor("epsc", [128, 1], F32))
        self.sgb = es.enter_context(nc.sbuf_tensor("sgb", [128, 8, T], BF16))
        self.vstg = es.enter_context(nc.sbuf_tensor("vstg", [128, 4, 1024], BF16))
        self.ps = es.enter_context(nc.psum_tensor("ps", [128, 8, T], F32))

    def new_res(self):
        self.r_xs = RL(KD)
        self.r_xn = RL(KD)
        self.r_hb = RL(KF)
        self.r_ys = RL(KD)
        self.r_rstd = RL(2)
        self.r_sg = RL(2)
        self.r_bank = RL(8)


def init_consts(nc, B, gcols_d):
    with ExitStack() as st:
        S = Stage(nc, st)
        sl = S.dsem("cld")
        rg, rgh, ro = Res(), Res(), Res()
        S.dma("sp", B.g[:], gcols_d, sl, wr=rg)
        S.op("dve", "memset", B.ones[:], 1.0 / D, wr=ro)
        S.op("dve", "memset", B.epsc[:], EPS, wr=Res())
        S.op("dve", "tensor_scalar", B.gh[:], B.g[:], 0.5, None, ALU.mult, rd=rg, wr=rgh)
        S.emit()


def load_x_and_norm(S, B, xin_d, t0, ldsem, gi):
    S.dma("sp", B.xs[:], xin_d[:, t0:t0 + T].rearrange("(c p) n -> p c n", p=128), ldsem, wr=B.r_xs)
    for c in range(KD):
        S.op("act", "activation", B.hb[:, c, :], B.xs[:, c, :], AF.Square, rd=B.r_xs[c], wr=B.r_hb[c])
        S.op("pe", "matmul", B.ps[:, 0, :], B.ones[:], B.hb[:, c, :], start=(c == 0), stop=(c == KD - 1),
             rd=B.r_hb[c], wr=B.r_bank[0], inc=(c == KD - 1))
    S.op("act", "activation", B.rstd[:, 0, :], B.ps[:, 0, :], AF.Sqrt, bias=B.epsc[:, 0:1], rd=B.r_bank[0], wr=B.r_rstd[0])
    S.op("dve", "reciprocal", B.rstd[:, 0, :], B.rstd[:, 0, :], rd=B.r_rstd[0], wr=B.r_rstd[0])
    for c in range(KD):
        S.op("dve", "scalar_tensor_tensor", B.xn[:, c, :], B.xs[:, c, :], B.g[:, gi, c:c + 1], B.rstd[:, 0, :], ALU.mult, ALU.mult,
             rd=(B.r_xs[c], B.r_rstd[0]), wr=B.r_xn[c])


def mlp_tail(S, ring, B, rhs, r_rhs, nk, w2_d, gcol, xout_d, t0, stsem):
    ps = B.ps
    for oc in range(KD):
        slot, rs = ring.load(w2_d[oc], nk * 128)
        wv = ring.wring[:, slot, 0:nk * 128].rearrange("p (k n) -> p k n", k=nk)
        bank = 5 + (oc % 2)
        for k in range(nk):
            S.op("pe", "matmul", ps[:, bank, :], wv[:, k, :], rhs[:, k, :], start=(k == 0), stop=(k == nk - 1),
                 rd=(rs, r_rhs[k]), wr=B.r_bank[bank], inc=(k == nk - 1))
        S.op("act", "activation", B.ys[:, oc, :], ps[:, bank, :], AF.Copy, rd=B.r_bank[bank], wr=B.r_ys[oc])
        if 'nostat' not in LC_SKIP:
            S.op("dve", "tensor_tensor", B.xn[:, oc, :], B.ys[:, oc, :], B.ys[:, oc, :], ALU.mult, rd=B.r_ys[oc], wr=B.r_xn[oc])
            S.op("pe", "matmul", ps[:, 0, :], B.ones[:], B.xn[:, oc, :], start=(oc == 0), stop=(oc == KD - 1),
                 rd=B.r_xn[oc], wr=B.r_bank[0], inc=(oc == KD - 1))
    if 'nostat' not in LC_SKIP:
        S.op("act", "activation", B.rstd[:, 1, :], ps[:, 0, :], AF.Sqrt, bias=B.epsc[:, 0:1], rd=B.r_bank[0], wr=B.r_rstd[1])
        S.op("dve", "reciprocal", B.rstd[:, 1, :], B.rstd[:, 1, :], rd=B.r_rstd[1], wr=B.r_rstd[1])
    for c in range(KD):
        if 'nostat' not in LC_SKIP:
            S.op("dve", "scalar_tensor_tensor", B.ys[:, c, :], B.ys[:, c, :], gcol[:, c:c + 1], B.rstd[:, 1, :], ALU.mult, ALU.mult,
                 rd=(B.r_ys[c], B.r_rstd[1]), wr=B.r_ys[c])
        S.op("pool", "tensor_tensor", B.xs[:, c, :], B.xs[:, c, :], B.ys[:, c, :], ALU.add, rd=(B.r_ys[c], B.r_xs[c]), wr=B.r_xs[c])
    S.dma("sp", xout_d[:, t0:t0 + T].rearrange("(c p) n -> p c n", p=128), B.xs[:], stsem, rd=B.r_xs)


def ffn_stage(nc, B, xin_d, xout_d, w1_d, w2_d, gi_a, gi_b, nt=NT):
    with ExitStack() as st:
        S = Stage(nc, st)
        B.new_res()
        ring = WRing(S, B.wring)
        ld = S.dsem("xld")
        stsem = S.dsem("xst")
        ps = B.ps
        for ti in range(nt):
            t0 = ti * T
            load_x_and_norm(S, B, xin_d, t0, ld, gi_a)
            for s in range(KF // 2):
                slot, rs = ring.load(w1_d[s], SLAB)
                wv = ring.wring[:, slot, :].rearrange("p (g k n) -> p g k n", g=2, k=KD)
                for jj in range(2):
                    j = 2 * s + jj
                    par = j % 2
                    bg, bu = 1 + par, 3 + par
                    for k in range(KD):
                        S.op("pe", "matmul", ps[:, bg, :], wv[:, 0, k, jj * 128:(jj + 1) * 128], B.xn[:, k, :], start=(k == 0), stop=(k == KD - 1),
                             rd=(rs, B.r_xn[k]), wr=B.r_bank[bg], inc=(k == KD - 1))
                    for k in range(KD):
                        S.op("pe", "matmul", ps[:, bu, :], wv[:, 1, k, jj * 128:(jj + 1) * 128], B.xn[:, k, :], start=(k == 0), stop=(k == KD - 1),
                             rd=(rs, B.r_xn[k]), wr=B.r_bank[bu], inc=(k == KD - 1))
                    S.op("act", "activation", B.sg[:, par, :], ps[:, bg, :], AF.Silu, rd=B.r_bank[bg], wr=B.r_sg[par])
                    S.op("dve", "tensor_tensor", B.hb[:, j, :], B.sg[:, par, :], ps[:, bu, :], ALU.mult, rd=(B.r_sg[par], B.r_bank[bu]), wr=B.r_hb[j])
            mlp_tail(S, ring, B, B.hb, B.r_hb, KF, w2_d, B.gh[:, gi_b, :], xout_d, t0, stsem)
        S.emit()


def proj_stage(nc, B, es_outer, xin_d, wm_d, wv_d, gi, zp_d, zlx_d, zlg_d, zq_d, zk_d, zv_d):
    with ExitStack() as st:
        S = Stage(nc, st)
        B.new_res()
        ring = WRing(S, B.wring)
        ld = S.dsem("xld")
        ps = B.ps
        NST = 8
        stsem = [S.dsem(f"zst{i}") for i in range(NST)]
        r_st = RL(NST)
        vsem = S.dsem("vst")
        r_v = RL(8)
        nst = 0
        for ti in range(NT):
            t0 = ti * T
            load_x_and_norm(S, B, xin_d, t0, ld, gi)
            ch = 0
            for s in range(7):
                slot, rs = ring.load(wm_d[s], SLAB)
                wv = ring.wring[:, slot, :].rearrange("p (k n) -> p k n", k=KD)
                for jj in range(4):
                    bank = 1 + (ch % 4)
                    for k in range(KD):
                        S.op("pe", "matmul", ps[:, bank, :], wv[:, k, jj * 128:(jj + 1) * 128], B.xn[:, k, :], start=(k == 0), stop=(k == KD - 1),
                             rd=(rs, B.r_xn[k]), wr=B.r_bank[bank], inc=(k == KD - 1))
                    sl = nst % NST
                    nst += 1
                    eng = "act" if ch % 2 == 0 else "dve"
                    if ch < 12:
                        dst = (zp_d, zlx_d, zlg_d)[ch // 4]
                        row = (ch % 4) * 128
                        o = B.ys[:, sl, :]
                        if eng == "act":
                            S.op("act", "activation", o, ps[:, bank, :], AF.Copy, rd=B.r_bank[bank], wr=r_st[sl])
                        else:
                            S.op("dve", "tensor_copy", o, ps[:, bank, :], rd=B.r_bank[bank], wr=r_st[sl])
                        S.dma("sp", dst[row:row + 128, t0:t0 + T], o, stsem[sl], rd=r_st[sl])
                    else:
                        isq = ch < 20
                        dst = zq_d if isq else zk_d
                        row = ((ch - 12) % 8) * 128
                        o = B.sgb[:, sl, :]
                        if isq:
                            S.op("act", "activation", o, ps[:, bank, :], AF.Copy, scale=0.125, rd=B.r_bank[bank], wr=r_st[sl])
                        elif eng == "act":
                            S.op("act", "activation", o, ps[:, bank, :], AF.Copy, rd=B.r_bank[bank], wr=r_st[sl])
                        else:
                            S.op("dve", "tensor_copy", o, ps[:, bank, :], rd=B.r_bank[bank], wr=r_st[sl])
                        S.dma("sp", dst[row:row + 128, t0:t0 + T], o, stsem[sl], rd=r_st[sl])
                    ch += 1
            for s in range(2):
                slot, rs = ring.load(wv_d[s], SLAB)
                wv = ring.wring[:, slot, :].rearrange("p (k n) -> p k n", k=KD)
                for tb in range(4):
                    bank = 1 + ((s * 4 + tb) % 4)
                    for k in range(KD):
                        S.op("pe", "matmul", ps[:, bank, :], B.xn[:, k, tb * 128:(tb + 1) * 128], wv[:, k, :], start=(k == 0), stop=(k == KD - 1),
                             rd=(rs, B.r_xn[k]), wr=B.r_bank[bank], inc=(k == KD - 1))
                    o = B.vstg[:, tb, s * 512:(s + 1) * 512]
                    if tb % 2 == 0:
                        S.op("act", "activation", o, ps[:, bank, :], AF.Copy, rd=B.r_bank[bank], wr=r_v[s * 4 + tb])
                    else:
                        S.op("dve", "tensor_copy", o, ps[:, bank, :], rd=B.r_bank[bank], wr=r_v[s * 4 + tb])
            S.dma("sp", zv_d[t0:t0 + T, :].rearrange("(b p) n -> p b n", p=128), B.vstg[:], vsem, rd=r_v)
        S.emit()


def build_LA(do_ffn=True, do_proj=True, nt=NT):
    nc = bass.Bass("TRN2", target_bir_lowering=False)
    dt = lambda n, sh, d, k: nc.dram_tensor(n, sh, d, kind=k).ap()
    xT = dt("xT", [D, TOK], F32, "ExternalInput")
    gcols = dt("gcols", [128, 6, KD], F32, "ExternalInput")
    w1 = dt("w1", [KF // 2, 128, SLAB], F32, "ExternalInput")
    w2 = dt("w2", [KD, 128, KF * 128], F32, "ExternalInput")
    wm = dt("wm", [7, 128, SLAB], F32, "ExternalInput")
    wvv = dt("wvv", [2, 128, SLAB], F32, "ExternalInput")
    x1T = dt("x1T", [D, TOK], F32, "ExternalOutput")
    zp = dt("zp", [512, TOK], F32, "ExternalOutput")
    zlx = dt("zlx", [512, TOK], F32, "ExternalOutput")
    zlg = dt("zlg", [512, TOK], F32, "ExternalOutput")
    zq = dt("zq", [1024, TOK], BF16, "ExternalOutput")
    zk = dt("zk", [1024, TOK], BF16, "ExternalOutput")
    zv = dt("zv", [TOK, 1024], BF16, "ExternalOutput")
    with ExitStack() as es:
        B = Bufs(nc, es)
        init_consts(nc, B, gcols)
        if do_ffn:
            ffn_stage(nc, B, xT, x1T, w1, w2, 0, 1, nt=nt)
        if do_proj:
            proj_stage(nc, B, es, x1T if (do_ffn and do_proj != 2) else xT, wm, wvv, 2, zp, zlx, zlg, zq, zk, zv)
    return nc


def tile_w1(W):
    return np.ascontiguousarray(W.reshape(KD, 128, 2, KF // 2, 256).transpose(3, 1, 2, 0, 4)).reshape(KF // 2, 128, SLAB)


def tile_w2(W, nk):
    return np.ascontiguousarray(W.reshape(nk, 128, KD, 128).transpose(2, 1, 0, 3)).reshape(KD, 128, nk * 128)


def tile_wcols(W):
    ns = W.shape[1] // 512
    return np.ascontiguousarray(W.reshape(KD, 128, ns, 512).transpose(2, 1, 0, 3)).reshape(ns, 128, SLAB)


def gcols_of(g):
    return np.ascontiguousarray(g.reshape(6, KD, 128).transpose(2, 0, 1))


NOPV = False
LC_SKIP = set()
SLOPES = [2.0 ** (-8.0 * (h + 1) / 16) for h in range(16)]
BRANCH_D = (1, 4, 16)
POOL_WINS = (2, 4, 8, 16)


def build_LB(do_mix=True, do_attn=True, branches=(0, 1, 2), nhp=8):
    nc = bass.Bass("TRN2", target_bir_lowering=False)
    dt = lambda n, sh, d, k: nc.dram_tensor(n, sh, d, kind=k).ap()
    qT = dt("qT", [1024, TOK], BF16, "ExternalInput")
    kT = dt("kT", [1024, 2 * TOK], BF16, "ExternalInput")
    vd = [dt(f"v{d}", [8, 128, 32 * 128], BF16, "ExternalInput") for d in BRANCH_D]
    tbl_d = dt("tbl", [128, 2, 256], F32, "ExternalInput")
    zp = dt("zp", [512, 16 + TOK], F32, "ExternalInput")
    zlx = dt("zlx", [512, 4 + TOK], F32, "ExternalInput")
    zlg = dt("zlg", [512, TOK], F32, "ExternalInput")
    prcp = dt("prcp", [128, 4, 16], F32, "ExternalInput")
    poolw = dt("poolw", [128, 4, 128], F32, "ExternalInput")
    pscale = dt("pscale", [128, 4], F32, "ExternalInput")
    convw = dt("convw", [128, 4, 4], F32, "ExternalInput")
    convb = dt("convb", [128, 4], F32, "ExternalInput")
    gatew = dt("gatew", [128, 2, 4, 128], F32, "ExternalInput")
    gateb = dt("gateb", [128, 2, 4], F32, "ExternalInput")
    lam = dt("lam", [128, 4], F32, "ExternalInput")
    yattn = dt("yattn", [1024, TOK], BF16, "ExternalOutput")
    ypool = dt("ypool", [512, TOK], BF16, "ExternalOutput")
    hG = dt("hG", [512, TOK], F32, "ExternalOutput")
    PG = dt("PG", [512, TOK], F32, "ExternalOutput")
    summ = dt("summ", [128, 2, 4], F32, "ExternalOutput")
    with ExitStack() as es:
        sb = lambda n, sh, d: es.enter_context(nc.sbuf_tensor(n, sh, d))
        with ExitStack() as st:
            S = Stage(nc, st)
            cs = S.dsem("cld")
            c_prcp = sb("c_prcp", [128, 4, 16], F32)
            c_pw = sb("c_pw", [128, 4, 128], F32)
            c_pwb = sb("c_pwb", [128, 4, 128], BF16)
            c_ps = sb("c_ps", [128, 4], F32)
            c_cw = sb("c_cw", [128, 4, 4], F32)
            c_cb = sb("c_cb", [128, 4], F32)
            c_gw = sb("c_gw", [128, 2, 4, 128], F32)
            c_gwb = sb("c_gwb", [128, 2, 4, 128], BF16)
            c_gb = sb("c_gb", [128, 2, 4], F32)
            c_lam = sb("c_lam", [128, 4], F32)
            c_sc = sb("c_sc", [128, 2, 4], F32)
            c_one = sb("c_one", [128, 1], F32)
            s_sum = sb("s_sum", [128, 2, 4], F32)
            rc = Res()
            for (t_, d_) in ((c_prcp, prcp), (c_pw, poolw), (c_ps, pscale), (c_cw, convw), (c_cb, convb), (c_gw, gatew), (c_gb, gateb), (c_lam, lam)):
                S.dma("sp", t_[:], d_, cs, wr=Res())
            S.ops["dve"].append((None, (), {}, ((cs, cs.n),), None))
            S.ops["act"].append((None, (), {}, ((cs, cs.n),), None))
            S.op("dve", "tensor_copy", c_pwb[:], c_pw[:], wr=rc)
            S.op("dve", "tensor_copy", c_gwb[:], c_gw[:], wr=rc)
            S.op("dve", "memset", c_one[:], 1.0, wr=rc)
            S.op("act", "activation", c_sc[:, 0, :], c_lam[:], AF.Exp, scale=-1.0, rd=rc, wr=rc)
            S.op("act", "activation", c_sc[:, 0, :], c_sc[:, 0, :], AF.Ln, bias=c_one[:, 0:1], rd=rc, wr=rc)
            S.op("dve", "tensor_scalar", c_sc[:, 1, :], c_sc[:, 0, :], -16.0, None, ALU.mult, rd=rc, wr=rc)
            S.op("dve", "tensor_scalar", c_sc[:, 0, :], c_sc[:, 0, :], -8.0, None, ALU.mult, rd=rc, wr=rc)
            W = TOK
            u = sb("m_u", [128, 16 + W], F32)
            s1 = sb("m_s1", [128, 16 + W], F32)
            s2 = sb("m_s2", [128, 16 + W], F32)
            ub = sb("m_ub", [128, W], BF16)
            ob = sb("m_ob", [128, W], BF16)
            xg = sb("m_xg", [128, W], F32)
            aa = sb("m_a", [128, W], F32)
            bb = sb("m_b", [128, W], F32)
            zz = sb("m_z", [128, W], F32)
            psm = es.enter_context(nc.psum_tensor("psm", [128, 8, 512], F32))
            lds = S.dsem("mld")
            sts = S.dsem("mst")
            r_u, r_s1, r_s2, r_ub, r_ob, r_xg, r_a, r_b, r_z = RL(9)
            r_pb = RL(8)
            r_sum = Res()
            for g in range(4):
                win = POOL_WINS[g]
                S.dma("sp", u[:], zp[g * 128:(g + 1) * 128, :], lds, wr=r_u)
                src, rsrc = u, r_u
                dsts = [(s1, r_s1), (s2, r_s2)]
                sh = 1
                k = 0
                lo = 1
                while sh < win:
                    dst, rdst = dsts[k % 2]
                    S.op("pool", "tensor_tensor", dst[:, lo + sh:16 + W], src[:, lo + sh:16 + W], src[:, lo:16 + W - sh], ALU.add, rd=rsrc, wr=rdst)
                    src, rsrc = dst, rdst
                    lo += sh
                    sh *= 2
                    k += 1
                S.op("dve", "scalar_tensor_tensor", zz[:], src[:, 16:16 + W], 1.0 / win, u[:, 16:16 + W], ALU.mult, ALU.subtract, rd=(rsrc, r_u), wr=r_z)
                S.op("dve", "tensor_tensor", xg[:, 0:16], src[:, 16:32], c_prcp[:, g, :], ALU.mult, rd=rsrc, wr=r_xg)
                S.op("dve", "tensor_tensor", zz[:, 0:16], xg[:, 0:16], u[:, 16:32], ALU.subtract, rd=(r_xg, r_u), wr=r_z)
                S.op("act", "activation", ub[:], zz[:], AF.Copy, rd=r_z, wr=r_ub)
                for tb in range(4):
                    bk = tb
                    S.op("pe", "matmul", psm[:, bk, :], c_pwb[:, g, :], ub[:, tb * 512:(tb + 1) * 512], start=True, stop=True, rd=r_ub, wr=r_pb[bk])
                    S.op("act", "activation", ob[:, tb * 512:(tb + 1) * 512], psm[:, bk, :], AF.Copy, scale=c_ps[:, g:g + 1], rd=r_pb[bk], wr=r_ob)
                S.dma("sp", ypool[g * 128:(g + 1) * 128, :], ob[:], sts, rd=r_ob)
            for c in range(4):
                S.dma("sp", u[:, 12:16 + W], zlx[c * 128:(c + 1) * 128, :], lds, wr=r_u)
                S.dma("sp", xg[:], zlg[c * 128:(c + 1) * 128, :], lds, wr=r_xg)
                S.op("dve", "tensor_scalar", s1[:, 16:16 + W], u[:, 13:13 + W], c_cw[:, c, 0:1], c_cb[:, c:c + 1], ALU.mult, ALU.add, rd=r_u, wr=r_s1)
                for j in range(1, 4):
                    S.op("dve", "scalar_tensor_tensor", s1[:, 16:16 + W], u[:, 13 + j:13 + j + W], c_cw[:, c, j:j + 1], s1[:, 16:16 + W], ALU.mult, ALU.add, rd=(r_u, r_s1), wr=r_s1)
                S.op("act", "activation", ub[:], s1[:, 16:16 + W], AF.Copy, rd=r_s1, wr=r_ub)
                for gi_, (dst, rdst) in enumerate(((aa, r_a), (bb, r_b))):
                    for tb in range(4):
                        bk = gi_ * 4 + tb
                        S.op("pe", "matmul", psm[:, bk, :], c_gwb[:, gi_, c, :], ub[:, tb * 512:(tb + 1) * 512], start=True, stop=True, rd=r_ub, wr=r_pb[bk])
                        S.op("act", "activation", dst[:, tb * 512:(tb + 1) * 512], psm[:, bk, :], AF.Sigmoid, bias=c_gb[:, gi_, c:c + 1], rd=r_pb[bk], wr=rdst)
                S.op("act", "activation", zz[:], aa[:], AF.Exp, scale=c_sc[:, 1, c:c + 1], rd=r_a, wr=r_z)
                S.op("act", "activation", aa[:], aa[:], AF.Exp, scale=c_sc[:, 0, c:c + 1], rd=r_a, wr=r_a)
                S.op("act", "activation", zz[:], zz[:], AF.Sqrt, scale=-1.0, bias=c_one[:, 0:1], rd=r_z, wr=r_z)
                S.op("dve", "tensor_tensor", bb[:], bb[:], s1[:, 16:16 + W], ALU.mult, rd=(r_b, r_s1), wr=r_b)
                S.op("dve", "tensor_tensor", bb[:], bb[:], zz[:], ALU.mult, rd=(r_b, r_z), wr=r_b)
                S.op("dve", "tensor_tensor_scan", s2[:, 0:W], aa[:], bb[:], 0.0, ALU.mult, ALU.add, rd=(r_a, r_b), wr=r_s2)
                S.op("dve", "memset", bb[:], 0.0, rd=r_s2, wr=r_b)
                S.op("dve", "tensor_tensor_scan", zz[:], aa[:], bb[:], 1.0, ALU.mult, ALU.add, rd=(r_a, r_b), wr=r_z)
                S.op("dve", "tensor_copy", s_sum[:, 0, c:c + 1], zz[:, W - 1:W], rd=r_z, wr=r_sum)
                S.op("dve", "tensor_copy", s_sum[:, 1, c:c + 1], s2[:, W - 1:W], rd=r_s2, wr=r_sum)
                S.op("dve", "tensor_tensor", bb[:], xg[:], xg[:], ALU.mult, rd=(r_xg, r_z), wr=r_b)
                S.op("dve", "tensor_scalar", bb[:], bb[:], 0.044715, 1.0, ALU.mult, ALU.add, rd=r_b, wr=r_b)
                S.op("dve", "tensor_tensor", bb[:], bb[:], xg[:], ALU.mult, rd=(r_b, r_xg), wr=r_b)
                S.op("act", "activation", bb[:], bb[:], AF.Sigmoid, scale=float(2.0 * np.sqrt(2.0 / np.pi)), rd=r_b, wr=r_b)
                S.op("dve", "tensor_tensor", xg[:], xg[:], bb[:], ALU.mult, rd=(r_b, r_xg), wr=r_xg)
                S.op("pool", "tensor_tensor", s2[:, 0:W], s2[:, 0:W], xg[:], ALU.mult, rd=(r_s2, r_xg), wr=r_s2)
                S.op("pool", "tensor_tensor", zz[:], zz[:], xg[:], ALU.mult, rd=(r_z, r_xg), wr=r_z)
                S.dma("sp", hG[c * 128:(c + 1) * 128, :], s2[:, 0:W], sts, rd=r_s2)
                S.dma("sp", PG[c * 128:(c + 1) * 128, :], zz[:], sts, rd=r_z)
            S.dma("sp", summ, s_sum[:], sts, rd=r_sum)
            if do_mix:
                S.emit()
        if do_attn:
            attn_stage(nc, sb, psm, qT, kT, vd, tbl_d, yattn, branches, nhp)
    return nc


def attn_stage(nc, sb, psa, qT, kT, vd, tbl_d, yattn, branches=(0, 1, 2), nhp=8):
    with ExitStack() as st:
        S = Stage(nc, st)
        cs = S.dsem("cld")
        tbl = sb("a_tbl", [128, 2, 256], F32)
        onesm = sb("a_ones", [128, 2, 128], BF16)
        qm = sb("a_q", [128, 2, 2, TOK], BF16)
        kt = sb("a_k", [128, 2, 2 * TOK], BF16)
        vt = [sb(f"a_v{d}", [128, 2, 32 * 128], BF16) for d in BRANCH_D]
        sbias = sb("a_sb", [128, 2, 512], F32)
        pT = sb("a_p", [128, 2, 512], BF16)
        acc = sb("a_acc", [128, 2, TOK], F32)
        yo = sb("a_yo", [128, 2, TOK], BF16)
        r_tbl, r_ones = Res(), Res()
        S.dma("sp", tbl[:], tbl_d, cs, wr=r_tbl)
        S.op("dve", "memset", onesm[:], 0.0, wr=r_ones)
        S.op("dve", "memset", onesm[:, 0, 0:64], 1.0, rd=r_ones, wr=r_ones)
        S.op("dve", "memset", onesm[:, 1, 64:128], 1.0, rd=r_ones, wr=r_ones)
        lq = [S.dsem(f"aq{i}") for i in range(2)]
        sy = [S.dsem(f"ay{i}") for i in range(2)]
        r_q, r_k, r_yo = RL(2), RL(2), RL(2)
        r_v = [RL(2) for _ in range(3)]
        r_sb, r_p = RL(2), RL(2)
        r_S, r_N = RL(2), RL(2)
        r_acc = Res()
        for bf in range(2):
            S.op("pool", "memset", qm[:, bf, :, :], 0.0, wr=r_q[bf])

        def loads(hp):
            bf = hp % 2
            S.dma("sp", qm[0:64, bf, 0, :], qT[hp * 128:hp * 128 + 64, :], lq[bf], wr=r_q[bf])
            S.dma("sp", qm[64:128, bf, 1, :], qT[hp * 128 + 64:hp * 128 + 128, :], lq[bf], wr=r_q[bf])
            S.dma("sp", kt[:, bf, :], kT[hp * 128:(hp + 1) * 128, :], lq[bf], wr=r_k[bf])
            for bi in range(3):
                S.dma("sp", vt[bi][:, bf, :], vd[bi][hp], lq[bf], wr=r_v[bi][bf])

        units = []
        for hp in range(nhp):
            for bi, d in enumerate(BRANCH_D):
                if bi not in branches:
                    continue
                nblk = 32 // d
                for r in range(d):
                    for m in range(nblk // 2, nblk):
                        units.append((hp, bi, d, r, m, nblk))

        def scores(u, i):
            hp, bi, d, r, m, nblk = u
            bf, pb = hp % 2, i % 2
            q0 = ((m - nblk // 2) * 128) * d + r
            for hh in range(2):
                for kb in range(2):
                    k0 = ((m - 1 + kb) * 128) * d + r
                    S.op("pe", "matmul", psa[:, pb, hh * 256 + kb * 128: hh * 256 + (kb + 1) * 128],
                         kt[:, bf, k0:k0 + 127 * d + 1:d], qm[:, bf, hh, q0:q0 + 127 * d + 1:d],
                         start=True, stop=True, rd=(r_k[bf], r_q[bf]), wr=r_S[pb], inc=(hh == 1 and kb == 1))

        def softmax(u, i):
            hp, bi, d, r, m, nblk = u
            pb = i % 2
            first = (m == nblk // 2)
            for hh in range(2):
                S.op("dve", "scalar_tensor_tensor", sbias[:, pb, hh * 256:(hh + 1) * 256], tbl[:, 1 if first else 0, :], float(SLOPES[2 * hp + hh] * d),
                     psa[:, pb, hh * 256:(hh + 1) * 256], ALU.mult, ALU.add, rd=(r_tbl, r_S[pb]), wr=r_sb[pb])
            S.op("act", "activation", pT[:, pb, :], sbias[:, pb, :], AF.Exp, rd=r_sb[pb], wr=r_p[pb])

        def pv(u, i):
            hp, bi, d, r, m, nblk = u
            bf, pb = hp % 2, i % 2
            nb = 2 + pb
            q0 = ((m - nblk // 2) * 128) * d + r
            n = 0
            for hh in range(2):
                for kb in range(2):
                    S.op("pe", "matmul", psa[:, nb, 128:256], onesm[:, hh, :], pT[:, pb, hh * 256 + kb * 128: hh * 256 + (kb + 1) * 128],
                         start=(n == 0), stop=(n == 3), rd=(r_p[pb], r_ones), wr=r_N[pb], inc=False)
                    n += 1
            for hh in range(2):
                for kb in range(2):
                    blk = r * nblk + (m - 1 + kb)
                    S.op("pe", "matmul", psa[64 * hh:64 * hh + 64, nb, 0:128], vt[bi][:, bf, blk * 128 + 64 * hh: blk * 128 + 64 * hh + 64],
                         pT[:, pb, hh * 256 + kb * 128: hh * 256 + (kb + 1) * 128], start=(kb == 0), stop=(kb == 1),
                         tile_position=(0, 64 * hh), rd=(r_v[bi][bf], r_p[pb]), wr=r_N[pb], inc=(hh == 1 and kb == 1))
            S.op("dve", "tensor_tensor", acc[:, :, q0:q0 + 127 * d + 1:d], acc[:, :, q0:q0 + 127 * d + 1:d],
                 psa[:, nb, 0:256].rearrange("p (w q) -> p w q", w=2), ALU.add, rd=(r_N[pb], r_acc), wr=r_acc)

        def finalize(hp):
            bf = hp % 2
            S.op("dve", "reciprocal", acc[:, 1, :], acc[:, 1, :], rd=r_acc, wr=r_acc)
            S.op("pool", "tensor_tensor", yo[:, bf, :], acc[:, 0, :], acc[:, 1, :], ALU.mult, rd=r_acc, wr=r_yo[bf])
            S.dma("sp", yattn[hp * 128:(hp + 1) * 128, :], yo[:, bf, :], sy[bf], rd=r_yo[bf])

        loads(0)
        cur_hp = -1
        for i, u in enumerate(units):
            hp = u[0]
            if hp != cur_hp:
                if hp + 1 < nhp:
                    loads(hp + 1)
                cur_hp = hp
            if i == 0:
                scores(u, i)
            new_hp = (i == 0) or (units[i - 1][0] != hp)
            if new_hp:
                S.op("pool", "memset", acc[:], 0.0, wr=r_acc)
            softmax(u, i)
            if i + 1 < len(units):
                scores(units[i + 1], i + 1)
            pv(u, i)
            if i + 1 == len(units) or units[i + 1][0] != hp:
                finalize(hp)
        S.emit()


def build_LC():
    nc = bass.Bass("TRN2", target_bir_lowering=False)
    dt = lambda n, sh, d, k: nc.dram_tensor(n, sh, d, kind=k).ap()
    xT = dt("xT", [D, TOK], F32, "ExternalInput")
    gcols = dt("gcols", [128, 6, KD], F32, "ExternalInput")
    wo = dt("wo", [KD, 128, KD * 128], F32, "ExternalInput")
    yattn = dt("yattn", [1024, TOK], BF16, "ExternalInput")
    ypool = dt("ypool", [512, TOK], BF16, "ExternalInput")
    hG = dt("hG", [512, TOK], F32, "ExternalInput")
    PG = dt("PG", [512, TOK], F32, "ExternalInput")
    sall = dt("sall", [128, 8, 2, 4], F32, "ExternalInput")
    onehot = dt("onehot", [128, 8], F32, "ExternalInput")
    x2T = dt("x2T", [D, TOK], F32, "ExternalOutput")
    with ExitStack() as es:
        B = Bufs(nc, es)
        init_consts(nc, B, gcols)
        s_all = es.enter_context(nc.sbuf_tensor("s_all", [128, 8, 2, 4], F32))
        s_oh = es.enter_context(nc.sbuf_tensor("s_oh", [128, 8], F32))
        hcur = es.enter_context(nc.sbuf_tensor("hcur", [128, 4], F32))
        hsel = es.enter_context(nc.sbuf_tensor("hsel", [128, 4], F32))
        with ExitStack() as st:
            S = Stage(nc, st)
            B.new_res()
            ring = WRing(S, B.wring)
            cs = S.dsem("cld")
            ld = S.dsem("xld")
            stsem = S.dsem("xst")
            lh = [S.dsem(f"lh{i}") for i in range(2)]
            rh = Res()
            S.dma("sp", s_all[:], sall, cs, wr=rh)
            S.dma("sp", s_oh[:], onehot, cs, wr=rh)
            S.op("dve", "memset", hcur[:], 0.0, rd=rh, wr=rh)
            S.op("dve", "memset", hsel[:], 0.0, rd=rh, wr=rh)
            for j in (range(8) if 'rec' not in LC_SKIP else ()):
                S.op("dve", "scalar_tensor_tensor", hsel[:], hcur[:], s_oh[:, j:j + 1], hsel[:], ALU.mult, ALU.add, rd=rh, wr=rh)
                S.op("dve", "tensor_tensor", hcur[:], hcur[:], s_all[:, j, 0, :], ALU.mult, rd=rh, wr=rh)
                S.op("dve", "tensor_tensor", hcur[:], hcur[:], s_all[:, j, 1, :], ALU.add, rd=rh, wr=rh)
            r_sgl = RL(2)
            for ti in range(1 if 'nt1' in LC_SKIP else (2 if 'nt2' in LC_SKIP else NT)):
                t0 = ti * T
                S.dma("sp", B.xs[:], xT[:, t0:t0 + T].rearrange("(c p) n -> p c n", p=128), ld, wr=B.r_xs)
                S.dma("sp", B.hb[:, 0:4, :], ypool[:, t0:t0 + T].rearrange("(c p) n -> p c n", p=128), ld, wr=B.r_hb[0:4])
                S.dma("sp", B.hb[:, 8:16, :], yattn[:, t0:t0 + T].rearrange("(c p) n -> p c n", p=128), ld, wr=B.r_hb[8:16])
                for c in (range(4) if 'lru' not in LC_SKIP else ()):
                    S.dma("sp", B.sg[:, 0, :], hG[c * 128:(c + 1) * 128, t0:t0 + T], lh[0], wr=r_sgl[0])
                    S.dma("sp", B.sg[:, 1, :], PG[c * 128:(c + 1) * 128, t0:t0 + T], lh[1], wr=r_sgl[1])
                    S.op("dve", "scalar_tensor_tensor", B.hb[:, 4 + c, :], B.sg[:, 1, :], hsel[:, c:c + 1], B.sg[:, 0, :], ALU.mult, ALU.add,
                         rd=(r_sgl, rh), wr=B.r_hb[4 + c])
                if 'tail' not in LC_SKIP:
                    mlp_tail(S, ring, B, B.hb, B.r_hb, KD, wo, B.g[:, 3, :], x2T, t0, stsem)
            S.emit()
    return nc


def build_FFN(gi_a, gi_b):
    nc = bass.Bass("TRN2", target_bir_lowering=False)
    dt = lambda n, sh, d, k: nc.dram_tensor(n, sh, d, kind=k).ap()
    xT = dt("xT", [D, TOK], F32, "ExternalInput")
    gcols = dt("gcols", [128, 6, KD], F32, "ExternalInput")
    w1 = dt("w1", [KF // 2, 128, SLAB], F32, "ExternalInput")
    w2 = dt("w2", [KD, 128, KF * 128], F32, "ExternalInput")
    x1T = dt("x1T", [D, TOK], F32, "ExternalOutput")
    with ExitStack() as es:
        B = Bufs(nc, es)
        init_consts(nc, B, gcols)
        ffn_stage(nc, B, xT, x1T, w1, w2, gi_a, gi_b)
    return nc


def build_PROJ():
    nc = bass.Bass("TRN2", target_bir_lowering=False)
    dt = lambda n, sh, d, k: nc.dram_tensor(n, sh, d, kind=k).ap()
    xT = dt("xT", [D, TOK], F32, "ExternalInput")
    gcols = dt("gcols", [128, 6, KD], F32, "ExternalInput")
    wm = dt("wm", [7, 128, SLAB], F32, "ExternalInput")
    wvv = dt("wvv", [2, 128, SLAB], F32, "ExternalInput")
    zp = dt("zp", [512, TOK], F32, "ExternalOutput")
    zlx = dt("zlx", [512, TOK], F32, "ExternalOutput")
    zlg = dt("zlg", [512, TOK], F32, "ExternalOutput")
    zq = dt("zq", [1024, TOK], BF16, "ExternalOutput")
    zk = dt("zk", [1024, TOK], BF16, "ExternalOutput")
    zv = dt("zv", [TOK, 1024], BF16, "ExternalOutput")
    with ExitStack() as es:
        B = Bufs(nc, es)
        init_consts(nc, B, gcols)
        proj_stage(nc, B, es, xT, wm, wvv, 2, zp, zlx, zlg, zq, zk, zv)
    return nc


def lb_inputs(c, zq, zk, zv, zp, zlx, zlg, P):
    bf = ml_dtypes.bfloat16
    prev = c - 1
    kprev = zk[prev] if c > 0 else np.zeros_like(zk[0])
    vprev = zv[prev] if c > 0 else np.zeros_like(zv[0])
    kT = np.ascontiguousarray(np.concatenate([kprev, zk[c]], axis=1))
    vext = np.concatenate([vprev, zv[c]], axis=0)
    d_in = dict(qT=zq[c], kT=kT)
    for d in BRANCH_D:
        nb = 32 // d
        a = vext.reshape(nb, 128, d, 8, 128).transpose(3, 1, 2, 0, 4)
        d_in[f"v{d}"] = np.ascontiguousarray(a).reshape(8, 128, 32 * 128)
    k = np.arange(128)[:, None]
    q = np.arange(128)[None, :]
    halves = []
    for kb in range(2):
        dist = q + 128 * (1 - kb) - k
        halves.append(np.where((dist >= 0) & (dist <= 128), -dist.astype(np.float32), np.float32(-1e9)))
    reg = np.concatenate(halves, axis=1).astype(np.float32)
    fst = reg.copy()
    if c == 0:
        fst[:, 0:128] = -1e9
    d_in["tbl"] = np.ascontiguousarray(np.stack([reg, fst], axis=1))
    zpe = np.zeros((512, 16 + TOK), np.float32)
    zpe[:, 16:] = zp[c]
    zle = np.zeros((512, 4 + TOK), np.float32)
    zle[:, 4:] = zlx[c]
    if c > 0:
        zpe[:, 1:16] = zp[prev][:, -15:]
        zle[:, 1:4] = zlx[prev][:, -3:]
    d_in["zp"] = zpe
    d_in["zlx"] = zle
    d_in["zlg"] = zlg[c]
    prcp = np.zeros((128, 4, 16), np.float32)
    for g, w in enumerate(POOL_WINS):
        pos = np.arange(1, 17, dtype=np.float32) if c == 0 else np.full(16, float(w), np.float32)
        prcp[:, g, :] = (np.float32(1.0) / np.minimum(pos, np.float32(w)))[None, :]
    d_in["prcp"] = prcp
    d_in.update(P)
    return d_in


def layer_params(inp, L):
    P = {}
    P["poolw"] = np.ascontiguousarray(inp["pool_w"][L].transpose(1, 0, 2))
    P["pscale"] = np.ascontiguousarray(inp["pool_scale"][L].reshape(4, 128).T)
    P["convw"] = np.ascontiguousarray(inp["lru_conv_w"][L].reshape(4, 4, 128).transpose(2, 1, 0))
    P["convb"] = np.ascontiguousarray(inp["lru_conv_b"][L].reshape(4, 128).T)
    gw = np.zeros((128, 2, 4, 128), np.float32)
    for g in range(2):
        for ch in range(4):
            gw[0:64, g, ch, 0:64] = inp["lru_gate_w"][L, g, 2 * ch]
            gw[64:128, g, ch, 64:128] = inp["lru_gate_w"][L, g, 2 * ch + 1]
    P["gatew"] = gw
    P["gateb"] = np.ascontiguousarray(inp["lru_gate_b"][L].reshape(2, 4, 128).transpose(2, 0, 1))
    P["lam"] = np.ascontiguousarray(inp["lru_lambda"][L].reshape(4, 128).T)
    return P


_PROGS = {}


def _prog(key, fn):
    if key not in _PROGS:
        _PROGS[key] = fn()
    return _PROGS[key]


def _run(nc, ins):
    res = run_bass_kernel_spmd(nc, ins, core_ids=list(range(NCORES)))
    return res.results


def kernel(**inp):
    inp = {k: np.asarray(v) for k, v in inp.items()}
    x = inp["x"][0]
    cores = range(NCORES)
    xT = [np.ascontiguousarray(x[c * TOK:(c + 1) * TOK].T) for c in cores]
    onehots = []
    for c in cores:
        oh = np.zeros((128, 8), np.float32)
        oh[:, c] = 1.0
        onehots.append(oh)
    for L in range(2):
        gc = gcols_of(inp["norm_g"][L])
        w1 = tile_w1(inp["ffn1_w_in"][L]); w2 = tile_w2(inp["ffn1_w_out"][L], KF)
        r = _run(_prog(("ffn", 0, 1), lambda: build_FFN(0, 1)), [dict(xT=xT[c], gcols=gc, w1=w1, w2=w2) for c in cores])
        xT = [r[c]["x1T"] for c in cores]
        del w1, w2
        wm = tile_wcols(inp["mix_w_in"][L][:, :3584]); wvv = tile_wcols(inp["mix_w_in"][L][:, 3584:])
        r = _run(_prog("proj", build_PROJ), [dict(xT=xT[c], gcols=gc, wm=wm, wvv=wvv) for c in cores])
        z = {k: [r[c][k] for c in cores] for k in ("zq", "zk", "zv", "zp", "zlx", "zlg")}
        P = layer_params(inp, L)
        r = _run(_prog("lb", build_LB), [lb_inputs(c, z["zq"], z["zk"], z["zv"], z["zp"], z["zlx"], z["zlg"], P) for c in cores])
        sall = np.ascontiguousarray(np.stack([r[c]["summ"] for c in cores], axis=1))
        wo = tile_w2(inp["mix_w_out"][L], KD)
        r2 = _run(_prog("lc", build_LC), [dict(xT=xT[c], gcols=gc, wo=wo, yattn=r[c]["yattn"], ypool=r[c]["ypool"], hG=r[c]["hG"], PG=r[c]["PG"],
                                              sall=sall, onehot=onehots[c]) for c in cores])
        xT = [r2[c]["x2T"] for c in cores]
        w1 = tile_w1(inp["ffn2_w_in"][L]); w2 = tile_w2(inp["ffn2_w_out"][L], KF)
        r = _run(_prog(("ffn", 4, 5), lambda: build_FFN(4, 5)), [dict(xT=xT[c], gcols=gc, w1=w1, w2=w2) for c in cores])
        xT = [r[c]["x1T"] for c in cores]
        del w1, w2
    out = np.concatenate([np.ascontiguousarray(xT[c].T) for c in cores], axis=0)[None]
    return out.astype(np.float32)
```
